# Optimizing a Trainium2 kernel written in Bass

```python
import math
import jax
import jax.numpy as jnp
from jax import lax
import numpy as np


D_MODEL = 4096
BATCH = 4
SEQ = 2048
DEPTH = 2

MEM_LEN = 256
MLA_HEADS = 16
MLA_NOPE = 128
MLA_ROPE = 64
MLA_V = 128
MLA_Q_RANK = 1024
MLA_KV_RANK = 512
SB_HEADS = 16
SB_DIM = 128
X_HEADS = 4
X_DIM = 128
FFN_HIDDEN = -(-(8 * D_MODEL) // (3 * 256)) * 256
Q_BLOCK = 128
ROPE_THETA = 10000.0
EPS = 1e-6
MLA_QK = MLA_NOPE + MLA_ROPE
SB_WIDTH = SB_HEADS * SB_DIM
MLA_WIDTH = MLA_HEADS * MLA_V
MIX_WIDTH = MLA_WIDTH + SB_WIDTH
IN_COLS = MLA_Q_RANK + MLA_KV_RANK + MLA_ROPE + 3 * SB_WIDTH

kernel_name = "hymba_mla_stickbreaking_hybrid"


def rmsnorm(x, g):
    xf = x.astype(jnp.float32)
    y = xf * lax.rsqrt(jnp.mean(xf * xf, axis=-1, keepdims=True) + EPS)
    return (y * g.astype(jnp.float32)).astype(x.dtype)


def rope(x, positions):
    half = x.shape[-1] // 2
    inv_freq = ROPE_THETA ** (-jnp.arange(half, dtype=jnp.float32) / half)
    ang = positions.astype(jnp.float32)[..., None] * inv_freq
    ang = ang.reshape(ang.shape[:2] + (1,) * (x.ndim - 3) + (half,))
    cos, sin = jnp.cos(ang), jnp.sin(ang)
    xf = x.astype(jnp.float32)
    x1, x2 = xf[..., :half], xf[..., half:]
    return jnp.concatenate([x1 * cos - x2 * sin, x1 * sin + x2 * cos], axis=-1).astype(x.dtype)


def mla_attention(cq, ckv, k_rope, positions, w_q_up, w_kv_up, g_q_lat, g_kv_lat, g_q, g_k):
    B, S = cq.shape[:2]
    q = (rmsnorm(cq, g_q_lat) @ w_q_up).reshape(B, S, MLA_HEADS, MLA_QK)
    kv = (rmsnorm(ckv, g_kv_lat) @ w_kv_up).reshape(B, S, MLA_HEADS, MLA_NOPE + MLA_V)
    k_nope, v = kv[..., :MLA_NOPE], kv[..., MLA_NOPE:]
    q_nope = rmsnorm(q[..., :MLA_NOPE], g_q[:MLA_NOPE])
    q_rope = rope(rmsnorm(q[..., MLA_NOPE:], g_q[MLA_NOPE:]), positions)
    k_nope = rmsnorm(k_nope, g_k[:MLA_NOPE])
    k_rope = rope(rmsnorm(k_rope, g_k[MLA_NOPE:]), positions)
    scale = 1.0 / math.sqrt(MLA_QK)
    outs = []
    for i in range(S // Q_BLOCK):
        lo, hi = i * Q_BLOCK, (i + 1) * Q_BLOCK
        s = (jnp.einsum('bqhd,bkhd->bhqk', q_nope[:, lo:hi], k_nope[:, :hi])
             + jnp.einsum('bqhr,bkr->bhqk', q_rope[:, lo:hi], k_rope[:, :hi])).astype(jnp.float32) * scale
        causal = jnp.arange(hi)[None, :] <= (lo + jnp.arange(Q_BLOCK))[:, None]
        p = jax.nn.softmax(jnp.where(causal, s, -jnp.inf), axis=-1)
        outs.append(jnp.einsum('bhqk,bkhd->bqhd', p.astype(v.dtype), v[:, :hi]))
    return jnp.concatenate(outs, axis=1).reshape(B, S, MLA_WIDTH)


def stick_breaking_attention(q, k, v):
    B, S = q.shape[:2]
    scale = 1.0 / math.sqrt(SB_DIM)
    outs = []
    for i in range(S // Q_BLOCK):
        lo, hi = i * Q_BLOCK, (i + 1) * Q_BLOCK
        z = jnp.einsum('bqhd,bkhd->bhqk', q[:, lo:hi], k[:, :hi]).astype(jnp.float32) * scale
        strict = jnp.arange(hi)[None, :] < (lo + jnp.arange(Q_BLOCK))[:, None]
        log_beta = jax.nn.log_sigmoid(z)
        log_keep = jnp.where(strict, jax.nn.log_sigmoid(-z), 0.0)
        log_rest = lax.cumsum(log_keep, axis=3, reverse=True) - log_keep
        a = jnp.where(strict, jnp.exp(log_beta + log_rest), 0.0)
        outs.append(jnp.einsum('bhqk,bkhd->bqhd', a.astype(v.dtype), v[:, :hi]))
    return jnp.concatenate(outs, axis=1).reshape(B, S, SB_WIDTH)


def memory_cross_attention(h, mem, w_xq, w_xkv, g_mem, g_xq, g_xk, w_xo):
    B, S = h.shape[:2]
    M = mem.shape[1]
    q = rmsnorm((h @ w_xq).reshape(B, S, X_HEADS, X_DIM), g_xq)
    kv = (rmsnorm(mem, g_mem) @ w_xkv).reshape(B, M, X_HEADS, 2 * X_DIM)
    k = rmsnorm(kv[..., :X_DIM], g_xk)
    v = kv[..., X_DIM:]
    s = jnp.einsum('bqhd,bmhd->bhqm', q, k).astype(jnp.float32) * (1.0 / math.sqrt(X_DIM))
    p = jax.nn.softmax(s, axis=-1)
    o = jnp.einsum('bhqm,bmhd->bqhd', p.astype(v.dtype), v).reshape(B, S, X_HEADS * X_DIM)
    return o @ w_xo


def _w(k, shape, fan_in):
    return jax.random.normal(k, shape, jnp.float32) * (fan_in ** -0.5)


def _g(k, shape):
    return 1.0 + 0.02 * jax.random.normal(k, shape, jnp.float32)


def setup_inputs(seed: int = 0) -> dict:
    key = jax.random.key(seed)
    ks = jax.random.split(key, 26)
    L = DEPTH
    x = jax.random.normal(ks[0], (BATCH, SEQ, D_MODEL), jnp.float32)
    mem = jax.random.normal(ks[1], (BATCH, MEM_LEN, D_MODEL), jnp.float32)
    offsets = jax.random.randint(ks[2], (BATCH, 1), 0, 1024, dtype=jnp.int32)
    positions = (jnp.arange(SEQ, dtype=jnp.int32)[None, :] + offsets).astype(jnp.int32)
    return {
        "x": x,
        "mem": mem,
        "positions": positions,
        "g_attn": _g(ks[3], (L, D_MODEL)),
        "w_in": _w(ks[4], (L, D_MODEL, IN_COLS), D_MODEL),
        "g_q_lat": _g(ks[5], (L, MLA_Q_RANK)),
        "g_kv_lat": _g(ks[6], (L, MLA_KV_RANK)),
        "w_q_up": _w(ks[7], (L, MLA_Q_RANK, MLA_HEADS * MLA_QK), MLA_Q_RANK),
        "w_kv_up": _w(ks[8], (L, MLA_KV_RANK, MLA_HEADS * (MLA_NOPE + MLA_V)), MLA_KV_RANK),
        "g_mla_q": _g(ks[9], (L, MLA_QK)),
        "g_mla_k": _g(ks[10], (L, MLA_QK)),
        "g_mla_out": _g(ks[11], (L, MLA_WIDTH)),
        "g_sb_out": _g(ks[12], (L, SB_WIDTH)),
        "w_out": _w(ks[13], (L, MIX_WIDTH, D_MODEL), MIX_WIDTH),
        "g_cross": _g(ks[14], (L, D_MODEL)),
        "g_mem": _g(ks[15], (L, D_MODEL)),
        "w_xq": _w(ks[16], (L, D_MODEL, X_HEADS * X_DIM), D_MODEL),
        "w_xkv": _w(ks[17], (L, D_MODEL, 2 * X_HEADS * X_DIM), D_MODEL),
        "g_xq": _g(ks[18], (L, X_DIM)),
        "g_xk": _g(ks[19], (L, X_DIM)),
        "w_xo": _w(ks[20], (L, X_HEADS * X_DIM, D_MODEL), X_HEADS * X_DIM),
        "g_ffn": _g(ks[21], (L, D_MODEL)),
        "w_gate": _w(ks[22], (L, D_MODEL, FFN_HIDDEN), D_MODEL),
        "w_up": _w(ks[23], (L, D_MODEL, FFN_HIDDEN), D_MODEL),
        "w_down": _w(ks[24], (L, FFN_HIDDEN, D_MODEL), FFN_HIDDEN),
    }


def reference(x, mem, positions, g_attn, w_in, g_q_lat, g_kv_lat, w_q_up, w_kv_up,
              g_mla_q, g_mla_k, g_mla_out, g_sb_out, w_out, g_cross, g_mem, w_xq, w_xkv,
              g_xq, g_xk, w_xo, g_ffn, w_gate, w_up, w_down):
    B, S = x.shape[:2]
    splits = np.cumsum([MLA_Q_RANK, MLA_KV_RANK, MLA_ROPE, SB_WIDTH, SB_WIDTH]).tolist()
    for l in range(DEPTH):
        n = rmsnorm(x, g_attn[l])
        proj = n @ w_in[l]
        cq, ckv, k_rope, q_sb, k_sb, v_sb = jnp.split(proj, splits, axis=-1)
        o_mla = mla_attention(cq, ckv, k_rope, positions, w_q_up[l], w_kv_up[l],
                              g_q_lat[l], g_kv_lat[l], g_mla_q[l], g_mla_k[l])
        o_sb = stick_breaking_attention(q_sb.reshape(B, S, SB_HEADS, SB_DIM),
                                        k_sb.reshape(B, S, SB_HEADS, SB_DIM),
                                        v_sb.reshape(B, S, SB_HEADS, SB_DIM))
        mixed = jnp.concatenate([rmsnorm(o_mla, g_mla_out[l]), rmsnorm(o_sb, g_sb_out[l])], axis=-1)
        x = x + mixed @ w_out[l]
        x = x + memory_cross_attention(rmsnorm(x, g_cross[l]), mem, w_xq[l], w_xkv[l],
                                       g_mem[l], g_xq[l], g_xk[l], w_xo[l])
        h = rmsnorm(x, g_ffn[l])
        x = x + (jax.nn.silu(h @ w_gate[l]) * (h @ w_up[l])) @ w_down[l]
    return x
```

```python
import math
from contextlib import ExitStack

import numpy as np
import concourse.bass as bass
import concourse.mybir as mybir
from concourse.bass_utils import run_bass_kernel_spmd

F32 = mybir.dt.float32
BF16 = mybir.dt.bfloat16
I32 = mybir.dt.int32
AF = mybir.ActivationFunctionType
ALU = mybir.AluOpType
AX = mybir.AxisListType

D = 4096
DC = 32
T = 1024
NT = 8
L = 2
MEM = 256
IN_COLS = 7744
FFN = 11008
FC = 86
EPS = 1e-6
ENGS = ["sync", "scalar", "vector", "gpsimd", "tensor"]


class Res:
    __slots__ = ("name", "w", "r", "excl")

    def __init__(self, name="", excl=False):
        self.name = name
        self.w = {}
        self.r = {}
        self.excl = excl


class Prog:
    def __init__(self, nc, stack, K=8, same_sync=True):
        self.nc = nc
        self.K = K
        self.same = same_sync
        self.ops = {e: [] for e in ENGS}
        self.cnt = {e: 0 for e in ENGS}
        self.waited = {e: {} for e in ENGS}
        self.sem = {}
        self.dcount = {}
        self.freed = {}
        for e in ENGS:
            self.sem["E:" + e] = stack.enter_context(nc.semaphore("se_" + e))
        for q in ["sync", "gpsimd", "scalar"]:
            self.dcount[q] = 0
            for k in range(K):
                self.sem["D:%s:%d" % (q, k)] = stack.enter_context(nc.semaphore("sd_%s%d" % (q, k)))
        self.sem["CC"] = stack.enter_context(nc.semaphore("s_cc"))
        self.ccount = 0

    def new_res(self, name=""):
        r = Res(name)
        r.r = dict(self.freed)
        return r

    def free_res(self, rs):
        for x in rs:
            for k, v in list(x.w.items()) + list(x.r.items()):
                self.freed[k] = max(self.freed.get(k, 0), v)

    def _deps(self, engine, reads, writes, extra=()):
        deps = {}

        def add(kv):
            k, v = kv
            if v > deps.get(k, 0):
                deps[k] = v

        for r in reads:
            for kv in r.w.items():
                add(kv)
        for w in writes:
            for kv in w.w.items():
                add(kv)
            for kv in w.r.items():
                add(kv)
        for kv in extra:
            add(kv)
        if not self.same:
            deps.pop("E:" + engine, None)
        wd = self.waited[engine]
        waits = []
        for k, v in deps.items():
            if wd.get(k, 0) < v:
                waits.append((k, v))
                wd[k] = v
        return waits

    def _mark(self, ev, reads, writes):
        k, v = ev
        for r in reads:
            if r.r.get(k, 0) < v:
                r.r[k] = v
        for w in writes:
            if w.w.get(k, 0) < v:
                w.w[k] = v
            w.r = {}

    def _issue(self, engine, waits, fn, ev, inc):
        e = getattr(self.nc, engine)
        for k, v in waits:
            e.wait_ge(self.sem[k], v)
        if fn is not None:
            ins = fn(e)
            ins.then_inc(self.sem[ev[0]], inc)

    def op(self, engine, fn, reads=(), writes=()):
        ex = [r for r in reads if r.excl]
        if ex:
            writes = list(writes) + ex
            reads = [r for r in reads if not r.excl]
        waits = self._deps(engine, reads, writes)
        self.cnt[engine] += 1
        ev = ("E:" + engine, self.cnt[engine])
        self._issue(engine, waits, fn, ev, 1)
        self._mark(ev, reads, writes)

    def dma(self, queue, out, in_, reads=(), writes=(), **kw):
        n = self.dcount[queue]
        self.dcount[queue] += 1
        k = n % self.K
        gen = n // self.K + 1
        key = "D:%s:%d" % (queue, k)
        extra = [(key, 16 * (gen - 1))] if gen > 1 else []
        waits = self._deps(queue, reads, writes, extra)
        ev = (key, 16 * gen)
        self._issue(queue, waits, lambda e: e.dma_start(out=out, in_=in_, **kw), ev, 16)
        self._mark(ev, reads, writes)

    def collective(self, fn, reads=(), writes=()):
        waits = self._deps("gpsimd", reads, writes)
        self.ccount += 1
        ev = ("CC", self.ccount)
        self._issue("gpsimd", waits, fn, ev, 1)
        self._mark(ev, reads, writes)

    def finish(self):
        waits = []
        for q, n in self.dcount.items():
            for k in range(self.K):
                cnt = (n - k + self.K - 1) // self.K if n > k else 0
                if cnt > 0:
                    waits.append(("D:%s:%d" % (q, k), 16 * cnt))
        for e in ENGS:
            if e != "sync" and self.cnt[e] > 0:
                waits.append(("E:" + e, self.cnt[e]))
        if self.ccount:
            waits.append(("CC", self.ccount))
        self._issue("sync", waits, None, None, 0)

    def emit(self):
        pass


WSPEC = {"w_in": (4096, 7744), "w_q_up": (1024, 3072), "w_kv_up": (512, 4096), "w_out": (4096, 4096),
         "w_xq": (4096, 512), "w_xkv": (4096, 1024), "w_xo": (512, 4096),
         "w_gate": (4096, 11008), "w_up": (4096, 11008), "w_down": (11008, 4096)}
PHASES = ["A", "B", "S", "M", "C", "D", "E"]
PHASE_W = {"A": ["w_in"], "B": ["w_kv_up", "w_q_up"], "S": [], "M": [], "C": ["w_out"],
           "D": ["w_xkv", "w_xq", "w_xo"], "E": ["w_gate", "w_up", "w_down"]}
GAINS = {"g_attn": D, "g_q_lat": 1024, "g_kv_lat": 512, "g_mla_q": 192, "g_mla_k": 192, "g_cross": D, "g_mem": D,
         "g_xq": 128, "g_xk": 128, "g_ffn": D}


def build_program(n_layers=L, stop=None, taps=(), same_sync=True):
    nc = bass.Bass("TRN2", target_bir_lowering=False)
    taps = set(taps)
    plan = []
    for l in range(n_layers):
        for ph in PHASES:
            plan.append((l, ph))
            if stop is not None and (l, ph) == tuple(stop):
                break
        else:
            continue
        break
    last = plan[-1]
    in_names = []

    def din(name, shape, dt=F32):
        in_names.append(name)
        return nc.dram_tensor(name, list(shape), dt, kind="ExternalInput")

    def dscr(name, shape, dt):
        kind = "ExternalOutput" if name in taps else "Internal"
        return nc.dram_tensor(name, list(shape), dt, kind=kind)

    x_in = din("x", [T, D])
    mem_in = din("mem", [MEM, D])
    pos_in = din("pos", [T, 1], I32)
    invf_in = din("invf", [1, 32])
    masks_in = din("masks", [4, 128, 128])
    consts_in = din("consts", [3, 128, 128])
    gfm_in = din("gfm", [128, L * 2 * 16])
    G = {k: din(k, [L, n]) for k, n in GAINS.items()}
    wneed = []
    for (l, ph) in plan:
        for w in PHASE_W[ph]:
            wneed.append((w, l))
    Wsh = {}
    for w in sorted(set(n for n, _ in wneed)):
        K_, N_ = WSPEC[w]
        Wsh[w] = din(w, [n_layers, K_ // 8, N_])
    y_out = nc.dram_tensor("y", [T, D], F32, kind="ExternalOutput")

    xres = dscr("xres", [T, D], F32)
    cq_s = dscr("cq_s", [T, 1024], F32)
    ckv_s = dscr("ckv_s", [T, 576], F32)
    qsbT = dscr("qsbT", [2048, T], BF16)
    qnT = dscr("qnT", [2048, T], BF16)
    qrT = dscr("qrT", [1024, T], BF16)
    MLA_ROWS = 2048 + 2048 + 64
    SB_ROWS = 2048 + 2048
    mla_own = dscr("mla_own", [MLA_ROWS, T], BF16)
    sb_own = dscr("sb_own", [SB_ROWS, T], BF16)
    mla_all = dscr("mla_all", [2 * MLA_ROWS, T], BF16)
    sb_all = dscr("sb_all", [2 * SB_ROWS, T], BF16)
    omT = dscr("omT", [2048, T], F32)
    osT = dscr("osT", [2048, T], F32)
    actT = dscr("actT", [FFN, T], BF16)
    tapx = {n: dscr(n, [T, D], F32) for n in ("tap_x1", "tap_x2") if n in taps}
    Wb = {}; Wf = {}; Wr = {}
    for (w, l) in wneed:
        K_, N_ = WSPEC[w]
        Wb[(w, l)] = nc.dram_tensor("%s_b%d" % (w, l), [K_ // 8, N_], BF16)
        Wf[(w, l)] = nc.dram_tensor("%s_f%d" % (w, l), [K_, N_], BF16, addr_space="Shared")
        Wr[(w, l)] = Res("%s%d" % (w, l))

    stack = ExitStack()
    with stack:
        P = Prog(nc, stack, same_sync=same_sync)

        def sb(name, shape, dt):
            return stack.enter_context(nc.sbuf_tensor(name, list(shape), dt))

        XT = sb("XT", [128, DC, T], BF16)
        XTr = [Res("XT%d" % i) for i in range(4)]
        WS = [sb("WS%d" % i, [128, 16384], BF16) for i in range(2)]
        WSr = [Res("WS%d" % i) for i in range(2)]
        ident = sb("ident", [128, 128], BF16)
        ones = sb("ones", [128, 128], BF16)
        Umat = sb("Umat", [128, 128], BF16)
        masks = sb("masks_sb", [128, 4, 128], BF16)
        gfm = sb("gfm_sb", [128, L * 2 * 16], F32)
        trig = sb("trig", [128, 2, NT, 32], F32)
        constr = Res("consts")
        PS = [stack.enter_context(nc.psum_tensor("ps%d" % i, [128, 512], F32)) for i in range(8)]
        PSr = [Res("ps%d" % i, excl=True) for i in range(8)]
        st = {"ps": 0, "ws": 0, "ev": 0}

        def next_ps():
            i = st["ps"]
            st["ps"] = (i + 1) % 8
            return PS[i], PSr[i]

        def ev_engine():
            st["ev"] ^= 1
            return "scalar" if st["ev"] else "vector"

        def bcast(ap, shape, axis):
            return ap.unsqueeze(axis).broadcast_to(list(shape))

        xr = [[Res("x%d_%d" % (i, j)) for j in range(8)] for i in range(NT)]
        cq_r = [Res() for _ in range(NT)]
        ckv_r = [Res() for _ in range(NT)]
        qsb_r = Res(); qn_r = Res(); qr_r = Res()
        mla_own_r = Res(); sb_own_r = Res(); mla_all_r = Res(); sb_all_r = Res()
        om_r = Res(); os_r = Res(); act_r = Res()

        def prep_weights(l, names):
            for w in names:
                if (w, l) not in Wb:
                    continue
                br = Res()
                K_, N_ = WSPEC[w]
                R_ = K_ // 8
                step = max(1, (4 << 20) // (N_ * 4))
                step = max(16, (step // 16) * 16)
                for r0 in range(0, R_, step):
                    r1 = min(R_, r0 + step)
                    P.dma("gpsimd", Wb[(w, l)][r0:r1, :], Wsh[w][l, r0:r1, :], writes=[br])

                def cc(e, w=w, l=l):
                    return e.collective_compute("AllGather", ALU.bypass, replica_groups=[list(range(8))],
                                                ins=[Wb[(w, l)].ap().opt()], outs=[Wf[(w, l)].ap().opt()])
                P.collective(cc, reads=[br], writes=[Wr[(w, l)]])

        def pair_gather(own, allt, own_r, all_r, rows):
            for r0 in range(0, rows, 1024):
                r1 = min(rows, r0 + 1024)

                def cc(e, r0=r0, r1=r1):
                    return e.collective_compute("AllGather", ALU.bypass, replica_groups=[[0, 1], [2, 3], [4, 5], [6, 7]],
                                                ins=[own[r0:r1, :].opt()], outs=[allt[2 * r0:2 * r0 + 2 * (r1 - r0), :].opt()])
                P.collective(cc, reads=[own_r], writes=[all_r])

        def arow(rk, r0, n, rows):
            k = r0 // 1024
            assert (r0 + n - 1) // 1024 == k
            nk = min(rows, (k + 1) * 1024) - k * 1024
            return 2048 * k + rk * nk + (r0 - k * 1024)

        P.dma("gpsimd", ident[:], consts_in[0], writes=[constr])
        P.dma("gpsimd", ones[:], consts_in[1], writes=[constr])
        P.dma("gpsimd", Umat[:], consts_in[2], writes=[constr])
        P.dma("gpsimd", masks[:], masks_in.ap().rearrange("m k q -> k m q"), writes=[constr])
        P.dma("sync", gfm[:], gfm_in.ap(), writes=[constr])
        prep_weights(0, ["w_in", "w_kv_up", "w_q_up", "w_out", "w_xkv", "w_xq", "w_xo", "w_gate", "w_up", "w_down"])

        with ExitStack() as ps_:
            posi = ps_.enter_context(nc.sbuf_tensor("posi", [128, NT], I32))
            posf = ps_.enter_context(nc.sbuf_tensor("posf", [128, NT], F32))
            invf = ps_.enter_context(nc.sbuf_tensor("invf_sb", [128, 32], F32))
            ang = ps_.enter_context(nc.sbuf_tensor("ang", [128, 2, NT, 32], F32))
            angn = ps_.enter_context(nc.sbuf_tensor("angn", [128, 2, NT, 32], F32))
            angi = ps_.enter_context(nc.sbuf_tensor("angi", [128, 2, NT, 32], I32))
            rr = [P.new_res() for _ in range(6)]
            P.dma("sync", posi[:], pos_in.ap().rearrange("(i p) o -> p (i o)", p=128), writes=[rr[0]],
                  allow_slow_non_contiguous=True)
            P.dma("sync", invf[:], invf_in.ap().partition_broadcast(128), writes=[rr[1]])
            P.op("vector", lambda e: e.tensor_copy(out=posf[:], in_=posi[:]), reads=[rr[0]], writes=[rr[2]])
            for i in range(NT):
                P.op("vector", lambda e, i=i: e.tensor_scalar(out=ang[:, 0, i, :], in0=invf[:], scalar1=posf[:, i:i + 1],
                                                              scalar2=None, op0=ALU.mult),
                     reads=[rr[1], rr[2]], writes=[rr[3]])
            P.op("vector", lambda e: e.tensor_scalar(out=ang[:, 1], in0=ang[:, 0], scalar1=0.5 * math.pi, scalar2=None,
                                                     op0=ALU.add), reads=[rr[3]], writes=[rr[3]])
            P.op("vector", lambda e: e.tensor_scalar(out=angn[:], in0=ang[:], scalar1=1.0 / (2 * math.pi), scalar2=None,
                                                     op0=ALU.mult), reads=[rr[3]], writes=[rr[4]])
            P.op("vector", lambda e: e.tensor_copy(out=angi[:], in_=angn[:]), reads=[rr[4]], writes=[rr[5]])
            P.op("vector", lambda e: e.tensor_copy(out=angn[:], in_=angi[:]), reads=[rr[5]], writes=[rr[4]])
            P.op("vector", lambda e: e.scalar_tensor_tensor(out=ang[:], in0=angn[:], scalar=-2 * math.pi, in1=ang[:],
                                                            op0=ALU.mult, op1=ALU.add), reads=[rr[4], rr[3]], writes=[rr[3]])
            P.op("vector", lambda e: e.tensor_scalar(out=angn[:], in0=ang[:], scalar1=math.pi, scalar2=-2 * math.pi,
                                                     op0=ALU.is_gt, op1=ALU.mult), reads=[rr[3]], writes=[rr[4]])
            P.op("vector", lambda e: e.tensor_tensor(out=ang[:], in0=ang[:], in1=angn[:], op=ALU.add),
                 reads=[rr[3], rr[4]], writes=[rr[3]])
            P.op("vector", lambda e: e.tensor_scalar(out=angn[:], in0=ang[:], scalar1=-math.pi, scalar2=2 * math.pi,
                                                     op0=ALU.is_lt, op1=ALU.mult), reads=[rr[3]], writes=[rr[4]])
            P.op("vector", lambda e: e.tensor_tensor(out=ang[:], in0=ang[:], in1=angn[:], op=ALU.add),
                 reads=[rr[3], rr[4]], writes=[rr[3]])
            P.op("scalar", lambda e: e.activation(out=trig[:], in_=ang[:], func=AF.Sin), reads=[rr[3]], writes=[constr])
            P.free_res(rr)

        def load_weight_block(slot, wkey, k0, kc, n0, ncols, col_off=0, width=None):
            width = width or ncols
            wt = Wf[wkey]
            view = WS[slot][:, 0:kc * width].rearrange("p (c n) -> p c n", n=width)
            step = 8
            for c0 in range(0, kc, step):
                c1 = min(kc, c0 + step)
                src = wt[k0 + c0 * 128:k0 + c1 * 128, n0:n0 + ncols].rearrange("(c p) n -> p c n", p=128)
                P.dma("sync", view[:, c0:c1, col_off:col_off + ncols], src, reads=[Wr[wkey]], writes=[WSr[slot]])
            return view

        def rstd_ops(src, tmp, dst, inv_n, r):
            P.op("vector", lambda e: e.tensor_scalar(out=tmp, in0=src, scalar1=inv_n, scalar2=EPS, op0=ALU.mult,
                                                     op1=ALU.add), reads=[r], writes=[r])
            P.op("scalar", lambda e: e.activation(out=tmp, in_=tmp, func=AF.Ln), reads=[r], writes=[r])
            P.op("scalar", lambda e: e.activation(out=dst, in_=tmp, func=AF.Exp, scale=-0.5), reads=[r], writes=[r])

        def transposes_to(src_ap_fn, nblk, src_res, dst_fn, dst_res, rows=128):
            for b0 in range(0, nblk, 8):
                b1 = min(nblk, b0 + 8)
                ps, ps_r = next_ps()
                pt = ps[:].bitcast(BF16)

                def tr(e, b0=b0, b1=b1, pt=pt):
                    ins = None
                    for b in range(b0, b1):
                        ins = e.transpose(out=pt[0:rows, (b - b0) * 128:(b - b0 + 1) * 128], in_=src_ap_fn(b),
                                          identity=ident[:])
                    return ins
                P.op("tensor", tr, reads=[src_res, constr], writes=[ps_r])
                src = pt[0:rows, 0:(b1 - b0) * 128].rearrange("p (c t) -> p c t", t=128)
                dst = dst_fn(b0, b1)
                if ev_engine() == "scalar":
                    P.op("scalar", lambda e, dst=dst, src=src: e.copy(out=dst, in_=src), reads=[ps_r], writes=[dst_res])
                else:
                    P.op("vector", lambda e, dst=dst, src=src: e.tensor_copy(out=dst, in_=src), reads=[ps_r],
                         writes=[dst_res])

        def rmsnorm_tm_to_XT(src_tile_ap, src_res, g_bc, width, chunk0, tile_i, bufs, ntok=128, tok_off=None):
            xt, xt_r, xnb, xnb_r, ssq, ssq_r = bufs
            nchunk = width // 128
            P.dma("sync", xt[:, 0:width], src_tile_ap, reads=src_res, writes=[xt_r])
            P.op("scalar", lambda e: e.activation(out=xnb[:, 0:width], in_=xt[:, 0:width], func=AF.Square,
                                                  accum_out=ssq[:, 0:1]),
                 reads=[xt_r], writes=[xnb_r, ssq_r])
            rstd_ops(ssq[:, 0:1], ssq[:, 1:2], ssq[:, 2:3], 1.0 / width, ssq_r)
            P.op("vector", lambda e: e.scalar_tensor_tensor(out=xnb[:, 0:width], in0=xt[:, 0:width], scalar=ssq[:, 2:3],
                                                            in1=g_bc[0][:, 0:width], op0=ALU.mult, op1=ALU.mult),
                 reads=[xt_r, ssq_r, g_bc[1]], writes=[xnb_r])
            toff = tile_i * 128 if tok_off is None else tok_off
            for g0 in range(0, nchunk, 8):
                g1 = min(nchunk, g0 + 8)
                grp = XTr[(chunk0 + g0) // 8]
                transposes_to(lambda b, g0=g0: xnb[:, (g0 + b) * 128:(g0 + b + 1) * 128], g1 - g0, xnb_r,
                              lambda b0, b1, g0=g0: XT[:, chunk0 + g0 + b0:chunk0 + g0 + b1, toff:toff + 128], grp)

        def xt_groups(chunk0, kc):
            return [XTr[g] for g in range(chunk0 // 8, (chunk0 + kc - 1) // 8 + 1)]

        def linear_tm(wkey, k0, kc, col_blocks, xt_chunk0, evac, tiles=range(NT), tok_of=lambda i: i * 128):
            slots = {}

            def load(bi):
                n0, ncols = col_blocks[bi]
                slot = st["ws"]; st["ws"] ^= 1
                slots[bi] = (slot, load_weight_block(slot, wkey, k0, kc, n0, ncols))
            load(0)
            for bi, (n0, ncols) in enumerate(col_blocks):
                if bi + 1 < len(col_blocks):
                    load(bi + 1)
                slot, wv = slots[bi]
                for i in tiles:
                    ps, ps_r = next_ps()

                    def mm(e, wv=wv, ps=ps, i=i, ncols=ncols):
                        ins = None
                        t0 = tok_of(i)
                        for c in range(kc):
                            ins = e.matmul(ps[:, 0:ncols], lhsT=XT[:, xt_chunk0 + c, t0:t0 + 128],
                                           rhs=wv[:, c, 0:ncols], start=(c == 0), stop=(c == kc - 1))
                        return ins
                    P.op("tensor", mm, reads=[WSr[slot]] + xt_groups(xt_chunk0, kc), writes=[ps_r])
                    evac(ps, ps_r, i, bi)

        def linear_fm(wkey, k0, kc, col_blocks, xt_chunk0, evac):
            slots = {}

            def load(bi):
                n0, ncols = col_blocks[bi]
                slot = st["ws"]; st["ws"] ^= 1
                slots[bi] = (slot, load_weight_block(slot, wkey, k0, kc, n0, ncols))
            load(0)
            for bi, (n0, ncols) in enumerate(col_blocks):
                if bi + 1 < len(col_blocks):
                    load(bi + 1)
                slot, wv = slots[bi]
                for r in range(ncols // 128):
                    for th in range(2):
                        ps, ps_r = next_ps()

                        def mm(e, wv=wv, ps=ps, r=r, th=th):
                            ins = None
                            for c in range(kc):
                                ins = e.matmul(ps[:, :], lhsT=wv[:, c, r * 128:(r + 1) * 128],
                                               rhs=XT[:, xt_chunk0 + c, th * 512:(th + 1) * 512],
                                               start=(c == 0), stop=(c == kc - 1))
                            return ins
                        P.op("tensor", mm, reads=[WSr[slot]] + xt_groups(xt_chunk0, kc), writes=[ps_r])
                        evac(ps, ps_r, n0 + r * 128, th)

        def copy_evac(eng, dst, src, ps_r, dst_r):
            if eng == "scalar":
                P.op("scalar", lambda e: e.copy(out=dst, in_=src), reads=[ps_r], writes=[dst_r])
            else:
                P.op("vector", lambda e: e.tensor_copy(out=dst, in_=src), reads=[ps_r], writes=[dst_r])

        def residual_linear(wkey, kc, xt_chunk0, x_src, x_dst, ph, tagn, l):
            xo = [ph.enter_context(nc.sbuf_tensor("%s_xo%d_L%d" % (tagn, i, l), [128, 512], F32)) for i in range(4)]
            xo_r = [P.new_res() for _ in range(4)]
            cnt = {"n": 0}

            def evac(ps, ps_r, i, bi):
                k = cnt["n"] % 4; cnt["n"] += 1
                P.dma("sync", xo[k][:, :], x_src[i * 128:(i + 1) * 128, bi * 512:(bi + 1) * 512],
                      reads=([xr[i][bi]] if x_src is not x_in else []), writes=[xo_r[k]])
                P.op("vector", lambda e, k=k, ps=ps: e.tensor_tensor(out=xo[k][:, :], in0=ps[:, :], in1=xo[k][:, :],
                                                                    op=ALU.add), reads=[ps_r, xo_r[k]], writes=[xo_r[k]])
                P.dma("sync", x_dst[i * 128:(i + 1) * 128, bi * 512:(bi + 1) * 512], xo[k][:, :], reads=[xo_r[k]],
                      writes=[xr[i][bi]])
            linear_tm(wkey, 0, kc, [(k * 512, 512) for k in range(8)], xt_chunk0, evac)
            return xo_r

        scale_mla = 1.0 / math.sqrt(192.0)
        scale_sb = 1.0 / math.sqrt(128.0)

        for (l, phase) in plan:
            x_src = x_in if l == 0 else xres
            is_last = (l == n_layers - 1)
            if phase == "A":
              with ExitStack() as ph:
                def psb(name, shape, dt):
                    return ph.enter_context(nc.sbuf_tensor("%s_L%d" % (name, l), list(shape), dt))
                xt = psb("xt", [128, D], F32); xnb = psb("xnb", [128, D], BF16)
                gbc = psb("gbc", [128, D], F32); ssq = psb("ssq", [128, 4], F32)
                ost = [psb("ost%d" % i, [128, 512], F32) for i in range(3)]
                osb = [psb("osb%d" % i, [128, 512], BF16) for i in range(3)]
                rs = [P.new_res() for _ in range(4 + 6)]
                xt_r, xnb_r, gbc_r, ssq_r = rs[0:4]
                ost_r = rs[4:7]; osb_r = rs[7:10]
                P.dma("sync", gbc[:], G["g_attn"][l].partition_broadcast(128), writes=[gbc_r])
                for i in range(NT):
                    rmsnorm_tm_to_XT(x_src[i * 128:(i + 1) * 128, :], (xr[i] if l > 0 else []), (gbc, gbc_r), D, 0, i,
                                     (xt, xt_r, xnb, xnb_r, ssq, ssq_r))
                cnt = {"f": 0, "b": 0}

                def evac_lat(ps, ps_r, i, bi):
                    k = cnt["f"] % 3; cnt["f"] += 1
                    ncols = 64 if bi == 3 else 512
                    copy_evac(ev_engine(), ost[k][:, 0:ncols], ps[:, 0:ncols], ps_r, ost_r[k])
                    if bi < 2:
                        P.dma("sync", cq_s[i * 128:(i + 1) * 128, bi * 512:(bi + 1) * 512], ost[k][:, :],
                              reads=[ost_r[k]], writes=[cq_r[i]])
                    elif bi == 2:
                        P.dma("sync", ckv_s[i * 128:(i + 1) * 128, 0:512], ost[k][:, :], reads=[ost_r[k]], writes=[ckv_r[i]])
                    else:
                        P.dma("sync", ckv_s[i * 128:(i + 1) * 128, 512:576], ost[k][:, 0:64], reads=[ost_r[k]],
                              writes=[ckv_r[i]])
                linear_tm(("w_in", l), 0, DC, [(0, 512), (512, 512), (1024, 512), (1536, 64)], 0, evac_lat)

                def evac_fm(dst_t, dst_res, row_off):
                    def f(ps, ps_r, n_abs, th):
                        k = cnt["b"] % 3; cnt["b"] += 1
                        copy_evac(ev_engine(), osb[k][:, :], ps[:, :], ps_r, osb_r[k])
                        row = n_abs - row_off
                        P.dma("sync", dst_t[row:row + 128, th * 512:(th + 1) * 512], osb[k][:, :], reads=[osb_r[k]],
                              writes=[dst_res])
                    return f
                linear_fm(("w_in", l), 0, DC, [(3648 + k * 512, 512) for k in range(4)], 0, evac_fm(sb_own, sb_own_r, 3648))
                vsb_view = sb_own[2048:4096, :].rearrange("(t a) c -> t (a c)", a=2)

                def evac_v(ps, ps_r, i, bi):
                    k = cnt["b"] % 3; cnt["b"] += 1
                    copy_evac(ev_engine(), osb[k][:, :], ps[:, :], ps_r, osb_r[k])
                    P.dma("sync", vsb_view[i * 128:(i + 1) * 128, bi * 512:(bi + 1) * 512], osb[k][:, :],
                          reads=[osb_r[k]], writes=[sb_own_r])
                linear_tm(("w_in", l), 0, DC, [(5696 + k * 512, 512) for k in range(4)], 0, evac_v)
                pair_gather(sb_own, sb_all, sb_own_r, sb_all_r, SB_ROWS)
                linear_fm(("w_in", l), 0, DC, [(1600 + k * 512, 512) for k in range(4)], 0, evac_fm(qsbT, qsb_r, 1600))
                P.free_res(rs)

            if phase == "B":
              with ExitStack() as ph:
                def psb(name, shape, dt):
                    return ph.enter_context(nc.sbuf_tensor("%s_L%d" % (name, l), list(shape), dt))
                ct = psb("ct", [128, 1024], F32); ctn = psb("ctn", [128, 1024], BF16)
                glat = psb("glat", [128, 1024], F32); gk = psb("gk", [128, 192], F32); gq = psb("gq", [128, 192], F32)
                ssq = psb("ssqB", [128, 8], F32)
                krn = psb("krn", [128, 64], F32); krt = psb("krt", [128, 4, 32], F32); kro = psb("kro", [128, 64], BF16)
                krTst = psb("krTst", [64, T], BF16)
                knraw = psb("knraw", [128, 16, 128], F32); junk = psb("junkB", [128, 1536], F32)
                ssqh = psb("ssqh", [128, 3, 16], F32)
                kst = psb("kst", [128, 16, 128], BF16); vst = psb("vst", [128, 2048], BF16)
                knTst = psb("knTst", [128, 16, 256], BF16)
                qraw = psb("qraw", [128, 8, 192], F32)
                qnb = psb("qnb", [128, 8, 128], BF16); qrn = psb("qrn", [128, 8, 64], F32)
                qrt = psb("qrt", [128, 4, 8, 32], F32); qro = psb("qro", [128, 8, 64], BF16)
                qnTst = psb("qnTst", [128, 8, 256], BF16); qrTst = psb("qrTst", [128, 4, 256], BF16)
                rs = [P.new_res() for _ in range(20)]
                (ct_r, ctn_r, glat_r, gk_r, ssq_r, krn_r, krt_r, kro_r, krTst_r, knraw_r, junk_r, ssqh_r, kst_r, vst_r,
                 knTst_r, qraw_r, qnb_r, qrn_r, qro_r, qTst_r) = rs
                P.dma("sync", glat[:, 0:512], G["g_kv_lat"][l].partition_broadcast(128), writes=[glat_r])
                P.dma("sync", gk[:], G["g_mla_k"][l].partition_broadcast(128), writes=[gk_r])
                P.dma("sync", gq[:], G["g_mla_q"][l].partition_broadcast(128), writes=[gk_r])
                for i in range(NT):
                    rmsnorm_tm_to_XT(ckv_s[i * 128:(i + 1) * 128, 0:512], [ckv_r[i]], (glat, glat_r), 512, 0, i,
                                     (ct, ct_r, ctn, ctn_r, ssq, ssq_r))
                    P.dma("sync", ct[:, 512:576], ckv_s[i * 128:(i + 1) * 128, 512:576], reads=[ckv_r[i]], writes=[ct_r])
                    P.op("scalar", lambda e: e.activation(out=krn[:], in_=ct[:, 512:576], func=AF.Square,
                                                          accum_out=ssq[:, 4:5]), reads=[ct_r], writes=[krn_r, ssq_r])
                    rstd_ops(ssq[:, 4:5], ssq[:, 5:6], ssq[:, 6:7], 1.0 / 64, ssq_r)
                    P.op("vector", lambda e: e.scalar_tensor_tensor(out=krn[:], in0=ct[:, 512:576], scalar=ssq[:, 6:7],
                                                                    in1=gk[:, 128:192], op0=ALU.mult, op1=ALU.mult),
                         reads=[ct_r, ssq_r, gk_r], writes=[krn_r])
                    sin_i = trig[:, 0, i, :]; cos_i = trig[:, 1, i, :]

                    def rope1(e, sin_i=sin_i, cos_i=cos_i):
                        e.tensor_tensor(out=krt[:, 0, :], in0=krn[:, 0:32], in1=cos_i, op=ALU.mult)
                        e.tensor_tensor(out=krt[:, 1, :], in0=krn[:, 32:64], in1=sin_i, op=ALU.mult)
                        e.tensor_tensor(out=krt[:, 2, :], in0=krn[:, 0:32], in1=sin_i, op=ALU.mult)
                        return e.tensor_tensor(out=krt[:, 3, :], in0=krn[:, 32:64], in1=cos_i, op=ALU.mult)
                    P.op("vector", rope1, reads=[krn_r, constr], writes=[krt_r])

                    def rope2(e):
                        e.tensor_tensor(out=kro[:, 0:32], in0=krt[:, 0, :], in1=krt[:, 1, :], op=ALU.subtract)
                        return e.tensor_tensor(out=kro[:, 32:64], in0=krt[:, 2, :], in1=krt[:, 3, :], op=ALU.add)
                    P.op("vector", rope2, reads=[krt_r], writes=[kro_r])
                    transposes_to(lambda b: kro[:, 0:64], 1, kro_r,
                                  lambda b0, b1, i=i: krTst[0:64, i * 128:(i + 1) * 128].rearrange("p (c t) -> p c t", t=128),
                                  krTst_r, rows=64)
                P.dma("sync", mla_own[4096:4160, :], krTst[:, :], reads=[krTst_r], writes=[mla_own_r])
                slot = st["ws"]; st["ws"] ^= 1
                wv = load_weight_block(slot, ("w_kv_up", l), 0, 4, 0, 4096)
                vm_view = mla_own[2048:4096, :].rearrange("(t a) c -> t (a c)", a=2)
                for i in range(NT):
                    for nb in range(8):
                        ps, ps_r = next_ps()

                        def mm(e, ps=ps, i=i, nb=nb):
                            ins = None
                            for c in range(4):
                                ins = e.matmul(ps[:, :], lhsT=XT[:, c, i * 128:(i + 1) * 128],
                                               rhs=wv[:, c, nb * 512:(nb + 1) * 512], start=(c == 0), stop=(c == 3))
                            return ins
                        P.op("tensor", mm, reads=[WSr[slot], XTr[0]], writes=[ps_r])
                        psv = ps[:, :].rearrange("p (h x) -> p h x", x=256)
                        P.op("scalar", lambda e, psv=psv, nb=nb: e.copy(out=knraw[:, 2 * nb:2 * nb + 2, :],
                                                                        in_=psv[:, :, 0:128]),
                             reads=[ps_r], writes=[knraw_r])
                        P.op("vector", lambda e, psv=psv, nb=nb: e.tensor_copy(
                            out=vst[:, nb * 256:(nb + 1) * 256].rearrange("p (h x) -> p h x", x=128),
                            in_=psv[:, :, 128:256]), reads=[ps_r], writes=[vst_r])
                    P.dma("sync", vm_view[i * 128:(i + 1) * 128, :], vst[:, :], reads=[vst_r], writes=[mla_own_r])
                    P.op("scalar", lambda e: e.activation(out=kst[:], in_=knraw[:], func=AF.Square),
                         reads=[knraw_r], writes=[kst_r])
                    P.op("vector", lambda e: e.tensor_reduce(out=ssqh[:, 0, :], in_=kst[:], axis=AX.X, op=ALU.add),
                         reads=[kst_r], writes=[ssqh_r])
                    rstd_ops(ssqh[:, 0, :], ssqh[:, 1, :], ssqh[:, 2, :], 1.0 / 128, ssqh_r)
                    P.op("vector", lambda e: e.tensor_tensor(out=knraw[:], in0=knraw[:],
                                                             in1=bcast(ssqh[:, 2, :], [128, 16, 128], 2), op=ALU.mult),
                         reads=[ssqh_r, knraw_r], writes=[knraw_r])
                    P.op("vector", lambda e: e.tensor_tensor(out=kst[:], in0=knraw[:],
                                                             in1=bcast(gk[:, 0:128], [128, 16, 128], 1), op=ALU.mult),
                         reads=[knraw_r, gk_r], writes=[kst_r])
                    transposes_to(lambda b: kst[:, b, :], 16, kst_r,
                                  lambda b0, b1, i=i: knTst[:, b0:b1, (i % 2) * 128:(i % 2 + 1) * 128], knTst_r)
                    if i % 2 == 1:
                        c0 = (i // 2) * 256
                        P.dma("sync", mla_own[0:2048, c0:c0 + 256].rearrange("(h d) t -> d h t", d=128), knTst[:],
                              reads=[knTst_r], writes=[mla_own_r])
                pair_gather(mla_own, mla_all, mla_own_r, mla_all_r, MLA_ROWS)
                if l + 1 < n_layers:
                    prep_weights(l + 1, ["w_in", "w_kv_up", "w_q_up", "w_out", "w_xkv", "w_xq", "w_xo", "w_gate", "w_up",
                                         "w_down"])
                P.dma("sync", glat[:, :], G["g_q_lat"][l].partition_broadcast(128), writes=[glat_r])
                for i in range(NT):
                    rmsnorm_tm_to_XT(cq_s[i * 128:(i + 1) * 128, :], [cq_r[i]], (glat, glat_r), 1024, 8, i,
                                     (ct, ct_r, ctn, ctn_r, ssq, ssq_r))
                for s_ in range(2):
                    slot = st["ws"]; st["ws"] ^= 1
                    wv = load_weight_block(slot, ("w_q_up", l), 0, 8, s_ * 1536, 1536)
                    for i in range(NT):
                        for blk in range(4):
                            ps, ps_r = next_ps()

                            def mm(e, ps=ps, i=i, blk=blk, wv=wv):
                                ins = None
                                for c in range(8):
                                    ins = e.matmul(ps[:, 0:384], lhsT=XT[:, 8 + c, i * 128:(i + 1) * 128],
                                                   rhs=wv[:, c, blk * 384:(blk + 1) * 384], start=(c == 0), stop=(c == 7))
                                return ins
                            P.op("tensor", mm, reads=[WSr[slot], XTr[1]], writes=[ps_r])
                            copy_evac(ev_engine(), qraw[:, 2 * blk:2 * blk + 2, :],
                                      ps[:, 0:384].rearrange("p (h x) -> p h x", x=192), ps_r, qraw_r)
                        junk3 = junk[:, 0:1536].rearrange("p (h x) -> p h x", x=192)
                        P.op("scalar", lambda e, junk3=junk3: e.activation(out=junk3, in_=qraw[:], func=AF.Square),
                             reads=[qraw_r], writes=[junk_r])
                        P.op("vector", lambda e, junk3=junk3: e.tensor_reduce(out=ssqh[:, 0, 0:8], in_=junk3[:, :, 0:128],
                                                                              axis=AX.X, op=ALU.add),
                             reads=[junk_r], writes=[ssqh_r])
                        P.op("vector", lambda e, junk3=junk3: e.tensor_reduce(out=ssqh[:, 0, 8:16], in_=junk3[:, :, 128:192],
                                                                              axis=AX.X, op=ALU.add),
                             reads=[junk_r], writes=[ssqh_r])
                        rstd_ops(ssqh[:, 0, 0:8], ssqh[:, 1, 0:8], ssqh[:, 2, 0:8], 1.0 / 128, ssqh_r)
                        rstd_ops(ssqh[:, 0, 8:16], ssqh[:, 1, 8:16], ssqh[:, 2, 8:16], 1.0 / 64, ssqh_r)
                        P.op("vector", lambda e: e.tensor_tensor(out=qraw[:, :, 0:128], in0=qraw[:, :, 0:128],
                                                                 in1=bcast(ssqh[:, 2, 0:8], [128, 8, 128], 2), op=ALU.mult),
                             reads=[ssqh_r, qraw_r], writes=[qraw_r])
                        P.op("vector", lambda e: e.tensor_tensor(out=qnb[:], in0=qraw[:, :, 0:128],
                                                                 in1=bcast(gq[:, 0:128], [128, 8, 128], 1), op=ALU.mult),
                             reads=[qraw_r, gk_r], writes=[qnb_r])
                        P.op("vector", lambda e: e.tensor_tensor(out=qraw[:, :, 128:192], in0=qraw[:, :, 128:192],
                                                                 in1=bcast(ssqh[:, 2, 8:16], [128, 8, 64], 2), op=ALU.mult),
                             reads=[ssqh_r, qraw_r], writes=[qraw_r])
                        P.op("vector", lambda e: e.tensor_tensor(out=qrn[:], in0=qraw[:, :, 128:192],
                                                                 in1=bcast(gq[:, 128:192], [128, 8, 64], 1), op=ALU.mult),
                             reads=[qraw_r, gk_r], writes=[qrn_r])
                        sin_b = bcast(trig[:, 0, i, :], [128, 8, 32], 1); cos_b = bcast(trig[:, 1, i, :], [128, 8, 32], 1)

                        def qrope1(e, sin_b=sin_b, cos_b=cos_b):
                            e.tensor_tensor(out=qrt[:, 0], in0=qrn[:, :, 0:32], in1=cos_b, op=ALU.mult)
                            e.tensor_tensor(out=qrt[:, 1], in0=qrn[:, :, 32:64], in1=sin_b, op=ALU.mult)
                            e.tensor_tensor(out=qrt[:, 2], in0=qrn[:, :, 0:32], in1=sin_b, op=ALU.mult)
                            return e.tensor_tensor(out=qrt[:, 3], in0=qrn[:, :, 32:64], in1=cos_b, op=ALU.mult)
                        P.op("vector", qrope1, reads=[qrn_r, constr], writes=[krt_r])

                        def qrope2(e):
                            e.tensor_tensor(out=qro[:, :, 0:32], in0=qrt[:, 0], in1=qrt[:, 1], op=ALU.subtract)
                            return e.tensor_tensor(out=qro[:, :, 32:64], in0=qrt[:, 2], in1=qrt[:, 3], op=ALU.add)
                        P.op("vector", qrope2, reads=[krt_r], writes=[qro_r])
                        transposes_to(lambda b: qnb[:, b, :], 8, qnb_r,
                                      lambda b0, b1, i=i: qnTst[:, b0:b1, (i % 2) * 128:(i % 2 + 1) * 128], qTst_r)
                        qro2 = qro[:].rearrange("p h x -> p (h x)")
                        transposes_to(lambda b, qro2=qro2: qro2[:, b * 128:(b + 1) * 128], 4, qro_r,
                                      lambda b0, b1, i=i: qrTst[:, b0:b1, (i % 2) * 128:(i % 2 + 1) * 128], qTst_r)
                        if i % 2 == 1:
                            c0 = (i // 2) * 256
                            P.dma("sync", qnT[s_ * 1024:(s_ + 1) * 1024, c0:c0 + 256].rearrange("(h d) t -> d h t", d=128),
                                  qnTst[:], reads=[qTst_r], writes=[qn_r])
                            P.dma("sync", qrT[s_ * 512:(s_ + 1) * 512, c0:c0 + 256].rearrange("(h d) t -> d h t", d=128),
                                  qrTst[:], reads=[qTst_r], writes=[qr_r])
                P.free_res(rs)

            if phase in ("S", "M"):
              is_sb = (phase == "S")
              with ExitStack() as ph:
                def psb(name, shape, dt):
                    return ph.enter_context(nc.sbuf_tensor("%s_L%d" % (name, l), list(shape), dt))
                tg = "s" if is_sb else "m"
                NS = 2
                NH = 3
                rs = []

                def nres(n):
                    out = [P.new_res() for _ in range(n)]
                    rs.extend(out)
                    return out
                kT = [psb("kT%s%d" % (tg, i), [128, 2, T], BF16) for i in range(NH)]
                vv = [psb("vv%s%d" % (tg, i), [128, 2, 8, 128], BF16) for i in range(NH)]
                kT_r = nres(NH); vv_r = nres(NH)
                qq = [psb("qq%s%d" % (tg, i), [128, 512], BF16) for i in range(NS)]; qq_r = nres(NS)
                pt_ = [[psb("pt%s%d_%d" % (tg, s_, i), [128, 512], BF16) for i in range(2)] for s_ in range(NS)]
                pt_r = [nres(2) for _ in range(NS)]
                osg = [psb("osg%s%d" % (tg, i), [128, 512], F32) for i in range(NS)]; osg_r = nres(NS)
                if is_sb:
                    sp_ = [[psb("sp%d_%d" % (s_, i), [128, 512], F32) for i in range(1)] for s_ in range(NS)]
                    lkb = [[psb("lkb%d_%d" % (s_, i), [128, 512], BF16) for i in range(1)] for s_ in range(NS)]
                    tt_ = [[psb("tt%d_%d" % (s_, i), [128, 512], F32) for i in range(1)] for s_ in range(NS)]
                    ssum = [psb("ssum%d" % s_, [128, 512], BF16) for s_ in range(NS)]
                    sp_r = [nres(1) for _ in range(NS)]; lkb_r = [nres(1) for _ in range(NS)]
                    tt_r = [nres(1) for _ in range(NS)]; ssum_r = nres(NS)
                    all_t, all_r, ROWS = sb_all, sb_all_r, SB_ROWS
                    q_t, q_r_ = qsbT, qsb_r
                    o_t, o_r_ = osT, os_r
                    mk = (2, 3)
                    Obank = [0, 1]; Zbank = [2, 3, 4]; Rbank = [5, 6, 7]
                else:
                    qr_ = [psb("qr%d" % i, [64, 512], BF16) for i in range(NS)]; qr_r2 = nres(NS)
                    krA = psb("krA", [64, 2, T], BF16); krA_r = nres(1)[0]
                    rden = [psb("rden%d" % i, [128, 512], F32) for i in range(NS)]; rden_r = nres(NS)
                    all_t, all_r, ROWS = mla_all, mla_all_r, MLA_ROWS
                    q_t, q_r_ = qnT, qn_r
                    o_t, o_r_ = omT, om_r
                    mk = (0, 1)
                    Obank = [0, 1]; Dbank = [2, 3]; Zbank = [4, 5, 6, 7]
                    for rk in range(2):
                        a0_ = arow(rk, 4096, 64, ROWS)
                        P.dma("sync", krA[:, rk, :], all_t[a0_:a0_ + 64, :], reads=[all_r], writes=[krA_r])
                kbase, vbase = 0, 2048
                zst = {"z": 0, "r": 0}
                loaded = set()

                def load_head(h):
                    if h in loaded or h >= 16:
                        return
                    loaded.add(h)
                    hb = h % NH
                    for rk in range(2):
                        a0_ = arow(rk, kbase + h * 128, 128, ROWS)
                        P.dma("sync", kT[hb][:, rk, :], all_t[a0_:a0_ + 128, :], reads=[all_r], writes=[kT_r[hb]])
                        for vh in range(2):
                            a0_ = arow(rk, vbase + vh * 1024, 1024, ROWS)
                            vview = all_t[a0_:a0_ + 1024, :].rearrange("(t a) c -> t (a c)", a=2)
                            P.dma("sync", vv[hb][:, rk, vh * 4:(vh + 1) * 4, :],
                                  vview[:, h * 128:(h + 1) * 128].rearrange("(b p) d -> p b d", p=128),
                                  reads=[all_r], writes=[vv_r[hb]])

                def stream(h, c, s_):
                    hb = h % NH
                    load_head(h)
                    P.dma("sync", qq[s_][:, :], q_t[h * 128:(h + 1) * 128, c * 512:(c + 1) * 512], reads=[q_r_],
                          writes=[qq_r[s_]])
                    if not is_sb:
                        P.dma("sync", qr_[s_][:, :], qrT[h * 64:(h + 1) * 64, c * 512:(c + 1) * 512], reads=[qr_r],
                              writes=[qr_r2[s_]])
                    yield
                    jmax = 8 * c + 7
                    O, O_r = PS[Obank[s_]], PSr[Obank[s_]]
                    if not is_sb:
                        Dn, Dn_r = PS[Dbank[s_]], PSr[Dbank[s_]]
                    js = list(range(jmax, -1, -1)) if is_sb else list(range(0, jmax + 1))
                    prev_col0 = None
                    for n_, j in enumerate(js):
                        first = (n_ == 0); lastj = (n_ == len(js) - 1)
                        i_min = max(4 * c, j // 2)
                        col0 = (i_min - 4 * c) * 128
                        diag = (j // 2 >= 4 * c)
                        rk = j % 2; lb = j // 2
                        ksl = kT[hb][:, rk, lb * 128:(lb + 1) * 128]
                        vsl = vv[hb][:, rk, lb, :]
                        zi = Zbank[zst["z"] % len(Zbank)]; zst["z"] += 1
                        Z, Z_r = PS[zi], PSr[zi]
                        pb = n_ % 2
                        ptb, ptb_r = pt_[s_][pb], pt_r[s_][pb]
                        if not is_sb:
                            def mmz(e):
                                e.matmul(Z[:, col0:512], lhsT=ksl, rhs=qq[s_][:, col0:512], start=True, stop=False)
                                return e.matmul(Z[:, col0:512], lhsT=krA[:, rk, lb * 128:(lb + 1) * 128],
                                                rhs=qr_[s_][:, col0:512], start=False, stop=True)
                            P.op("tensor", mmz, reads=[kT_r[hb], qq_r[s_], qr_r2[s_], krA_r], writes=[Z_r])
                            P.op("scalar", lambda e: e.activation(out=ptb[:, col0:512], in_=Z[:, col0:512], func=AF.Exp,
                                                                  scale=scale_mla), reads=[Z_r], writes=[ptb_r])
                            if diag:
                                P.op("vector", lambda e: e.tensor_tensor(
                                    out=ptb[:, col0:col0 + 128], in0=ptb[:, col0:col0 + 128], in1=masks[:, mk[rk], :],
                                    op=ALU.mult), reads=[ptb_r, constr], writes=[ptb_r])
                            yield

                            def mmo(e):
                                e.matmul(O[:, col0:512], lhsT=vsl, rhs=ptb[:, col0:512], start=first, stop=lastj)
                                return e.matmul(Dn[:, col0:512], lhsT=ones[:], rhs=ptb[:, col0:512], start=first, stop=lastj)
                            P.op("tensor", mmo, reads=[vv_r[hb], ptb_r, constr], writes=[O_r, Dn_r])
                            yield
                        else:
                            sb_ = 0
                            spb, spb_r = sp_[s_][sb_], sp_r[s_][sb_]
                            lk, lk_r = lkb[s_][sb_], lkb_r[s_][sb_]
                            ttb, ttb_r = tt_[s_][sb_], tt_r[s_][sb_]
                            ss, ss_r = ssum[s_], ssum_r[s_]
                            P.op("tensor", lambda e: e.matmul(Z[:, col0:512], lhsT=ksl, rhs=qq[s_][:, col0:512], start=True,
                                                              stop=True), reads=[kT_r[hb], qq_r[s_]], writes=[Z_r])
                            P.op("scalar", lambda e: e.activation(out=spb[:, col0:512], in_=Z[:, col0:512], func=AF.Exp,
                                                                  scale=-scale_sb), reads=[Z_r], writes=[spb_r])
                            P.op("scalar", lambda e: e.activation(out=spb[:, col0:512], in_=spb[:, col0:512], func=AF.Ln,
                                                                  bias=1.0), reads=[spb_r], writes=[spb_r])
                            P.op("vector", lambda e: e.scalar_tensor_tensor(
                                out=lk[:, col0:512], in0=Z[:, col0:512], scalar=scale_sb, in1=spb[:, col0:512],
                                op0=ALU.mult, op1=ALU.add), reads=[Z_r, spb_r], writes=[lk_r])
                            if diag:
                                P.op("vector", lambda e: e.tensor_tensor(
                                    out=lk[:, col0:col0 + 128], in0=lk[:, col0:col0 + 128], in1=masks[:, mk[rk], :],
                                    op=ALU.mult), reads=[lk_r, constr], writes=[lk_r])
                            yield
                            ri = Rbank[zst["r"] % len(Rbank)]; zst["r"] += 1
                            R, R_r = PS[ri], PSr[ri]

                            def mmr(e):
                                ins = e.matmul(R[:, col0:512], lhsT=Umat[:], rhs=lk[:, col0:512], start=True, stop=first)
                                if not first:
                                    ins = e.matmul(R[:, prev_col0:512], lhsT=ones[:], rhs=ss[:, prev_col0:512],
                                                   start=False, stop=True)
                                return ins
                            P.op("tensor", mmr, reads=[lk_r, ss_r, constr], writes=[R_r])
                            P.op("vector", lambda e: e.tensor_tensor(out=ttb[:, col0:512], in0=R[:, col0:512],
                                                                     in1=spb[:, col0:512], op=ALU.add),
                                 reads=[R_r, spb_r], writes=[ttb_r])
                            a0 = 0 if first else col0
                            if first and col0 > 0:
                                P.op("vector", lambda e: e.memset(ptb[:, 0:col0], 0.0), writes=[ptb_r])
                            P.op("scalar", lambda e: e.activation(out=ptb[:, col0:512], in_=ttb[:, col0:512], func=AF.Exp,
                                                                  scale=-1.0), reads=[ttb_r], writes=[ptb_r])
                            if diag:
                                P.op("vector", lambda e: e.tensor_tensor(
                                    out=ptb[:, col0:col0 + 128], in0=ptb[:, col0:col0 + 128], in1=masks[:, mk[rk], :],
                                    op=ALU.mult), reads=[ptb_r, constr], writes=[ptb_r])
                            if not lastj:
                                if first:
                                    P.op("vector", lambda e: e.tensor_copy(out=ss[:, col0:512], in_=lk[:, col0:512]),
                                         reads=[lk_r], writes=[ss_r])
                                else:
                                    def upd(e):
                                        ins = e.tensor_tensor(out=ss[:, prev_col0:512], in0=ss[:, prev_col0:512],
                                                              in1=lk[:, prev_col0:512], op=ALU.add)
                                        if col0 < prev_col0:
                                            ins = e.tensor_copy(out=ss[:, col0:prev_col0], in_=lk[:, col0:prev_col0])
                                        return ins
                                    P.op("vector", upd, reads=[lk_r], writes=[ss_r])
                            yield
                            P.op("tensor", lambda e: e.matmul(O[:, a0:512], lhsT=vsl, rhs=ptb[:, a0:512], start=first,
                                                              stop=lastj), reads=[vv_r[hb], ptb_r], writes=[O_r])
                            yield
                            prev_col0 = col0
                    if is_sb:
                        copy_evac("scalar", osg[s_][:, :], O[:, :], O_r, osg_r[s_])
                    else:
                        P.op("vector", lambda e: e.reciprocal(out=rden[s_][:, :], in_=Dn[:, :]), reads=[Dn_r],
                             writes=[rden_r[s_]])
                        P.op("vector", lambda e: e.tensor_tensor(out=osg[s_][:, :], in0=O[:, :], in1=rden[s_][:, :],
                                                                 op=ALU.mult), reads=[O_r, rden_r[s_]], writes=[osg_r[s_]])
                    P.dma("sync", o_t[h * 128:(h + 1) * 128, c * 512:(c + 1) * 512], osg[s_][:, :], reads=[osg_r[s_]],
                          writes=[o_r_])
                    yield

                pending = [(h, c) for h in range(16) for c in (1, 0)]
                active = [None] * NS
                ahead = [None] * NS
                while pending or any(a is not None for a in active):
                    for s_ in range(NS):
                        if active[s_] is None and pending:
                            h, c = pending[0]
                            busy_heads = set(x for x in ahead if x is not None)
                            if all(hh > h - NH for hh in busy_heads):
                                pending.pop(0)
                                active[s_] = stream(h, c, s_)
                                ahead[s_] = h
                        if active[s_] is not None:
                            try:
                                next(active[s_])
                            except StopIteration:
                                active[s_] = None
                                ahead[s_] = None
                P.free_res(rs)

            if phase == "C":
              with ExitStack() as ph:
                def psb(name, shape, dt):
                    return ph.enter_context(nc.sbuf_tensor("%s_L%d" % (name, l), list(shape), dt))
                oT = psb("oT", [128, 16, 512], F32)
                sqb = [psb("sqb%d" % i, [128, 512], BF16) for i in range(2)]
                rstd = psb("rstdC", [128, 512], F32)
                rs = [P.new_res() for _ in range(4)]
                oT_r, rstd_r = rs[0], rs[1]; sqb_r = rs[2:4]
                for g_, (src_t, src_r) in enumerate(((omT, om_r), (osT, os_r))):
                    for th in range(2):
                        for c4 in range(4):
                            P.dma("sync", oT[:, c4 * 4:(c4 + 1) * 4, :],
                                  src_t[c4 * 512:(c4 + 1) * 512, th * 512:(th + 1) * 512].rearrange("(c p) t -> p c t", p=128),
                                  reads=[src_r], writes=[oT_r])
                        ps, ps_r = next_ps()
                        for c in range(16):
                            k = c % 2
                            P.op("scalar", lambda e, k=k, c=c: e.activation(out=sqb[k][:, :], in_=oT[:, c, :], func=AF.Square),
                                 reads=[oT_r], writes=[sqb_r[k]])
                            P.op("tensor", lambda e, ps=ps, k=k, c=c: e.matmul(ps[:, :], lhsT=ones[:], rhs=sqb[k][:, :],
                                                                               start=(c == 0), stop=(c == 15)),
                                 reads=[sqb_r[k], constr], writes=[ps_r])
                        P.op("vector", lambda e, ps=ps: e.tensor_scalar(out=rstd[:, :], in0=ps[:, :], scalar1=1.0 / 2048,
                                                                        scalar2=EPS, op0=ALU.mult, op1=ALU.add),
                             reads=[ps_r], writes=[rstd_r])
                        P.op("scalar", lambda e: e.activation(out=rstd[:, :], in_=rstd[:, :], func=AF.Ln), reads=[rstd_r],
                             writes=[rstd_r])
                        P.op("scalar", lambda e: e.activation(out=rstd[:, :], in_=rstd[:, :], func=AF.Exp, scale=-0.5),
                             reads=[rstd_r], writes=[rstd_r])
                        for c in range(16):
                            gcol = (l * 2 + g_) * 16 + c
                            xc = g_ * 16 + c
                            P.op("vector", lambda e, c=c, gcol=gcol, xc=xc, th=th: e.scalar_tensor_tensor(
                                out=XT[:, xc, th * 512:(th + 1) * 512], in0=oT[:, c, :], scalar=gfm[:, gcol:gcol + 1],
                                in1=rstd[:, :], op0=ALU.mult, op1=ALU.mult),
                                reads=[oT_r, rstd_r, constr], writes=[XTr[xc // 8]])
                xo_r = residual_linear(("w_out", l), DC, 0, x_src, xres, ph, "C", l)
                P.free_res(rs + xo_r)
                if "tap_x1" in tapx and l == 0:
                    for i in range(NT):
                        P.dma("sync", tapx["tap_x1"][i * 128:(i + 1) * 128, :], xres[i * 128:(i + 1) * 128, :], reads=xr[i])

            if phase == "D":
              with ExitStack() as ph:
                def psb(name, shape, dt):
                    return ph.enter_context(nc.sbuf_tensor("%s_L%d" % (name, l), list(shape), dt))
                xt = psb("xtD", [128, D], F32); xnb = psb("xnbD", [128, D], BF16)
                gbc = psb("gbcD", [128, D], F32); ssq = psb("ssqD", [128, 4], F32)
                gx = psb("gxD", [128, 2, 128], F32)
                kxT = psb("kxT", [128, 4, MEM], BF16); vx = psb("vx", [128, 2, 512], BF16)
                kvraw = psb("kvraw", [128, 512], F32); junk = psb("junkD", [128, 512], F32)
                ssqh = psb("ssqhD", [128, 3, 4], F32)
                knb = psb("knbD", [128, 4, 128], BF16)
                qxT = psb("qxT", [128, 4, 128], BF16)
                pex = [psb("pex%d" % i, [128, 2, 128], BF16) for i in range(2)]
                rdx = psb("rdx", [128, 128], F32)
                XO = psb("XO", [128, 4, T], BF16)
                rs = [P.new_res() for _ in range(16)]
                (xt_r, xnb_r, gbc_r, ssq_r, gx_r, kxT_r, vx_r, kvraw_r, junk_r, ssqh_r, knb_r, qxT_r, pex_r0, pex_r1, rdx_r,
                 XO_r) = rs
                pex_r = [pex_r0, pex_r1]
                P.dma("sync", gx[:, 0, :], G["g_xq"][l].partition_broadcast(128), writes=[gx_r])
                P.dma("sync", gx[:, 1, :], G["g_xk"][l].partition_broadcast(128), writes=[gx_r])
                P.dma("sync", gbc[:], G["g_mem"][l].partition_broadcast(128), writes=[gbc_r])
                for mt in range(2):
                    rmsnorm_tm_to_XT(mem_in[mt * 128:(mt + 1) * 128, :], [], (gbc, gbc_r), D, 0, mt,
                                     (xt, xt_r, xnb, xnb_r, ssq, ssq_r))

                def headnorm(raw_ap, nh, gsel, out_ap):
                    j3 = junk[:, 0:nh * 128].rearrange("p (h x) -> p h x", x=128)
                    P.op("scalar", lambda e: e.activation(out=j3, in_=raw_ap, func=AF.Square), reads=[kvraw_r],
                         writes=[junk_r])
                    P.op("vector", lambda e: e.tensor_reduce(out=ssqh[:, 0, 0:nh], in_=j3, axis=AX.X, op=ALU.add),
                         reads=[junk_r], writes=[ssqh_r])
                    rstd_ops(ssqh[:, 0, 0:nh], ssqh[:, 1, 0:nh], ssqh[:, 2, 0:nh], 1.0 / 128, ssqh_r)
                    P.op("vector", lambda e: e.tensor_tensor(out=raw_ap, in0=raw_ap,
                                                             in1=bcast(ssqh[:, 2, 0:nh], [128, nh, 128], 2), op=ALU.mult),
                         reads=[ssqh_r, kvraw_r], writes=[kvraw_r])
                    P.op("vector", lambda e: e.tensor_tensor(out=out_ap, in0=raw_ap,
                                                             in1=bcast(gx[:, gsel, :], [128, nh, 128], 1), op=ALU.mult),
                         reads=[kvraw_r, gx_r], writes=[knb_r])

                def evac_kv(ps, ps_r, mt, bi):
                    psv = ps[:, :].rearrange("p (h x) -> p h x", x=256)
                    P.op("scalar", lambda e: e.copy(out=kvraw[:, 0:256].rearrange("p (h x) -> p h x", x=128),
                                                    in_=psv[:, :, 0:128]), reads=[ps_r], writes=[kvraw_r])
                    P.op("vector", lambda e: e.tensor_copy(out=vx[:, mt, bi * 256:(bi + 1) * 256].rearrange(
                        "p (h x) -> p h x", x=128), in_=psv[:, :, 128:256]), reads=[ps_r], writes=[vx_r])
                    headnorm(kvraw[:, 0:256].rearrange("p (h x) -> p h x", x=128), 2, 1, knb[:, 0:2, :])
                    transposes_to(lambda b: knb[:, b, :], 2, knb_r,
                                  lambda b0, b1: kxT[:, 2 * bi + b0:2 * bi + b1, mt * 128:(mt + 1) * 128], kxT_r)
                linear_tm(("w_xkv", l), 0, DC, [(0, 512), (512, 512)], 0, evac_kv, tiles=range(2))
                P.dma("sync", gbc[:], G["g_cross"][l].partition_broadcast(128), writes=[gbc_r])
                for i in range(NT):
                    rmsnorm_tm_to_XT(xres[i * 128:(i + 1) * 128, :], xr[i], (gbc, gbc_r), D, 0, i,
                                     (xt, xt_r, xnb, xnb_r, ssq, ssq_r))
                scale_x = 1.0 / math.sqrt(128.0)

                def evac_q(ps, ps_r, i, bi):
                    copy_evac("scalar", kvraw[:, :], ps[:, :], ps_r, kvraw_r)
                    headnorm(kvraw[:, :].rearrange("p (h x) -> p h x", x=128), 4, 0, knb[:, :, :])
                    transposes_to(lambda b: knb[:, b, :], 4, knb_r, lambda b0, b1: qxT[:, b0:b1, :], qxT_r)
                    for hx in range(4):
                        S_, S_r = next_ps()

                        def mms(e, S_=S_, hx=hx):
                            ins = None
                            for mt in range(2):
                                ins = e.matmul(S_[:, mt * 128:(mt + 1) * 128], lhsT=kxT[:, hx, mt * 128:(mt + 1) * 128],
                                               rhs=qxT[:, hx, :], start=True, stop=True)
                            return ins
                        P.op("tensor", mms, reads=[kxT_r, qxT_r], writes=[S_r])
                        pb = hx % 2
                        P.op("scalar", lambda e, S_=S_, pb=pb: e.activation(
                            out=pex[pb][:].rearrange("p a b -> p (a b)"), in_=S_[:, 0:256], func=AF.Exp, scale=scale_x),
                            reads=[S_r], writes=[pex_r[pb]])
                        O_, O_r = next_ps()

                        def mmo(e, O_=O_, pb=pb, hx=hx):
                            for mt in range(2):
                                e.matmul(O_[:, 0:128], lhsT=vx[:, mt, hx * 128:(hx + 1) * 128], rhs=pex[pb][:, mt, :],
                                         start=(mt == 0), stop=(mt == 1))
                            ins = None
                            for mt in range(2):
                                ins = e.matmul(O_[:, 128:256], lhsT=ones[:], rhs=pex[pb][:, mt, :], start=(mt == 0),
                                               stop=(mt == 1))
                            return ins
                        P.op("tensor", mmo, reads=[vx_r, pex_r[pb], constr], writes=[O_r])
                        P.op("vector", lambda e, O_=O_: e.reciprocal(out=rdx[:, :], in_=O_[:, 128:256]), reads=[O_r],
                             writes=[rdx_r])
                        P.op("vector", lambda e, O_=O_, hx=hx, i=i: e.tensor_tensor(
                            out=XO[:, hx, i * 128:(i + 1) * 128], in0=O_[:, 0:128], in1=rdx[:, :], op=ALU.mult),
                            reads=[O_r, rdx_r], writes=[XO_r])
                linear_tm(("w_xq", l), 0, DC, [(0, 512)], 0, evac_q)
                P.op("vector", lambda e: e.tensor_copy(out=XT[:, 0:4, :], in_=XO[:, :, :]), reads=[XO_r], writes=[XTr[0]])
                xo_r = residual_linear(("w_xo", l), 4, 0, xres, xres, ph, "D", l)
                P.free_res(rs + xo_r)
                if "tap_x2" in tapx and l == 0:
                    for i in range(NT):
                        P.dma("sync", tapx["tap_x2"][i * 128:(i + 1) * 128, :], xres[i * 128:(i + 1) * 128, :], reads=xr[i])

            if phase == "E":
              with ExitStack() as ph:
                def psb(name, shape, dt):
                    return ph.enter_context(nc.sbuf_tensor("%s_L%d" % (name, l), list(shape), dt))
                xt = psb("xtE", [128, D], F32); xnb = psb("xnbE", [128, D], BF16)
                gbc = psb("gbcE", [128, D], F32); ssq = psb("ssqE", [128, 4], F32)
                sg = [psb("sg%d" % i, [128, 512], F32) for i in range(2)]
                ab = [psb("ab%d" % i, [128, 512], BF16) for i in range(3)]
                rs = [P.new_res() for _ in range(4 + 2 + 3)]
                xt_r, xnb_r, gbc_r, ssq_r = rs[0:4]
                sg_r = rs[4:6]; ab_r = rs[6:9]
                P.dma("sync", gbc[:], G["g_ffn"][l].partition_broadcast(128), writes=[gbc_r])
                for i in range(NT):
                    rmsnorm_tm_to_XT(xres[i * 128:(i + 1) * 128, :], xr[i], (gbc, gbc_r), D, 0, i,
                                     (xt, xt_r, xnb, xnb_r, ssq, ssq_r))
                NB = FFN // 256
                slots = {}

                def loadgu(b):
                    slot = st["ws"]; st["ws"] ^= 1
                    load_weight_block(slot, ("w_gate", l), 0, DC, b * 256, 256, col_off=0, width=512)
                    v = load_weight_block(slot, ("w_up", l), 0, DC, b * 256, 256, col_off=256, width=512)
                    slots[b] = (slot, v)
                loadgu(0)
                cnt = 0
                for b in range(NB):
                    if b + 1 < NB:
                        loadgu(b + 1)
                    slot, wv = slots[b]
                    for r in range(2):
                        for th in range(2):
                            pg, pg_r = next_ps()
                            pu, pu_r = next_ps()

                            def mm(e, wv=wv, pg=pg, pu=pu, r=r, th=th):
                                ins = None
                                for c in range(DC):
                                    ins = e.matmul(pg[:, :], lhsT=wv[:, c, r * 128:(r + 1) * 128],
                                                   rhs=XT[:, c, th * 512:(th + 1) * 512], start=(c == 0), stop=(c == DC - 1))
                                for c in range(DC):
                                    ins = e.matmul(pu[:, :], lhsT=wv[:, c, 256 + r * 128:256 + (r + 1) * 128],
                                                   rhs=XT[:, c, th * 512:(th + 1) * 512], start=(c == 0), stop=(c == DC - 1))
                                return ins
                            P.op("tensor", mm, reads=[WSr[slot]] + XTr, writes=[pg_r, pu_r])
                            k2 = cnt % 2; k3 = cnt % 3; cnt += 1
                            P.op("scalar", lambda e, k2=k2, pg=pg: e.activation(out=sg[k2][:, :], in_=pg[:, :], func=AF.Silu),
                                 reads=[pg_r], writes=[sg_r[k2]])
                            P.op("vector", lambda e, k2=k2, k3=k3, pu=pu: e.tensor_tensor(
                                out=ab[k3][:, :], in0=pu[:, :], in1=sg[k2][:, :], op=ALU.mult),
                                reads=[pu_r, sg_r[k2]], writes=[ab_r[k3]])
                            row = b * 256 + r * 128
                            P.dma("sync", actT[row:row + 128, th * 512:(th + 1) * 512], ab[k3][:, :], reads=[ab_r[k3]],
                                  writes=[act_r])
                P.free_res(rs)
              if phase == "E":
               with ExitStack() as ph:
                def psb(name, shape, dt):
                    return ph.enter_context(nc.sbuf_tensor("%s_L%d" % (name, l), list(shape), dt))
                wd = [psb("wd%d" % i, [128, 4, 512], BF16) for i in range(3)]
                aT = [psb("aT%d" % i, [128, 4, T], BF16) for i in range(3)]
                xo = [psb("xoE%d" % i, [128, 512], F32) for i in range(4)]
                rs = [P.new_res() for _ in range(3 + 3 + 4)]
                wd_r = rs[0:3]; aT_r = rs[3:6]; xo_r = rs[6:10]
                y_dst = y_out if is_last else xres
                groups = [(j0, min(FC, j0 + 4)) for j0 in range(0, FC, 4)]
                cnt = 0
                xcnt = 0
                for nb in range(8):
                    for gi, (j0, j1) in enumerate(groups):
                        k3 = cnt % 3; cnt += 1
                        nj = j1 - j0
                        P.dma("sync", wd[k3][:, 0:nj, :],
                              Wf[("w_down", l)][j0 * 128:j1 * 128, nb * 512:(nb + 1) * 512].rearrange("(c p) n -> p c n", p=128),
                              reads=[Wr[("w_down", l)]], writes=[wd_r[k3]])
                        P.dma("sync", aT[k3][:, 0:nj, :], actT[j0 * 128:j1 * 128, :].rearrange("(c p) t -> p c t", p=128),
                              reads=[act_r], writes=[aT_r[k3]])

                        def mmd(e, k3=k3, j0=j0, nj=nj):
                            ins = None
                            for jj in range(nj):
                                for i in range(NT):
                                    ins = e.matmul(PS[i][:, :], lhsT=aT[k3][:, jj, i * 128:(i + 1) * 128], rhs=wd[k3][:, jj, :],
                                                   start=(j0 + jj == 0), stop=(j0 + jj == FC - 1))
                            return ins
                        P.op("tensor", mmd, reads=[wd_r[k3], aT_r[k3]], writes=PSr)
                    for i in range(NT):
                        k = xcnt % 4; xcnt += 1
                        P.dma("sync", xo[k][:, :], xres[i * 128:(i + 1) * 128, nb * 512:(nb + 1) * 512], reads=[xr[i][nb]],
                              writes=[xo_r[k]])
                        P.op("vector", lambda e, k=k, i=i: e.tensor_tensor(out=xo[k][:, :], in0=PS[i][:, :], in1=xo[k][:, :],
                                                                          op=ALU.add), reads=[PSr[i], xo_r[k]], writes=[xo_r[k]])
                        P.dma("sync", y_dst[i * 128:(i + 1) * 128, nb * 512:(nb + 1) * 512], xo[k][:, :], reads=[xo_r[k]],
                              writes=[xr[i][nb]])
                P.free_res(rs)

        if last != (n_layers - 1, "E"):
            pass
        P.finish()
        P.emit()
    return nc, in_names


_CACHE = {}


def _consts():
    ident = np.eye(128, dtype=np.float32)
    ones = np.ones((128, 128), np.float32)
    kk = np.arange(128)
    U = (kk[:, None] > kk[None, :]).astype(np.float32)
    return np.stack([ident, ones, U])


def _masks(h):
    k = np.arange(128)[:, None]
    q = np.arange(128)[None, :]
    le = (k <= q).astype(np.float32)
    lt = (k < q).astype(np.float32)
    o = np.ones((128, 128), np.float32)
    z = np.zeros((128, 128), np.float32)
    if h == 0:
        return np.stack([le, z, lt, z])
    return np.stack([o, le, o, lt])


def make_in_maps(inputs, in_names, n_layers=L):
    x = np.asarray(inputs["x"]); mem = np.asarray(inputs["mem"]); pos = np.asarray(inputs["positions"])
    half = 32
    invf = (10000.0 ** (-np.arange(half, dtype=np.float32) / half)).astype(np.float32)[None, :]
    gfm = np.stack([np.asarray(inputs["g_mla_out"]), np.asarray(inputs["g_sb_out"])], axis=1)
    gfm = np.ascontiguousarray(gfm.reshape(L, 2, 16, 128).transpose(3, 0, 1, 2).reshape(128, L * 2 * 16))
    consts = _consts()
    maps = []
    for b in range(4):
        for h in range(2):
            c = 2 * b + h
            xc = np.ascontiguousarray(x[b].reshape(8, 2, 128, D)[:, h].reshape(T, D))
            pc = np.ascontiguousarray(pos[b].reshape(8, 2, 128)[:, h].reshape(T, 1)).astype(np.int32)
            m = {"x": xc, "mem": np.ascontiguousarray(mem[b]), "pos": pc, "invf": invf, "masks": _masks(h),
                 "consts": consts, "gfm": gfm}
            for k in GAINS:
                m[k] = np.ascontiguousarray(np.asarray(inputs[k]))
            for w, (K_, N_) in WSPEC.items():
                if w in in_names:
                    R_ = K_ // 8
                    m[w] = np.ascontiguousarray(np.asarray(inputs[w])[:n_layers, c * R_:(c + 1) * R_, :])
            maps.append({k: v for k, v in m.items() if k in in_names})
    return maps


def kernel(**inputs):
    if "nc" not in _CACHE:
        _CACHE["nc"] = build_program()
    nc, in_names = _CACHE["nc"]
    maps = make_in_maps(inputs, in_names)
    res = run_bass_kernel_spmd(nc, maps, core_ids=list(range(8)))
    out = np.empty((4, 2048, D), np.float32)
    for b in range(4):
        ob = out[b].reshape(8, 2, 128, D)
        for h in range(2):
            ob[:, h] = np.asarray(res.results[2 * b + h]["y"]).reshape(8, 128, D)
    return out
```
